# Optimizing a Trainium2 kernel written in Bass

```python
import jax, jax.numpy as jnp
from jax import lax
import numpy as np

D_MODEL = 1024
BATCH = 4
SEQ = 4096
DEPTH = 1

N_META = 16
D_RNN = 1024
N_RNN_HEADS = 4
RNN_HEAD_DIM = D_RNN // N_RNN_HEADS
RNN_CONV_WIDTH = 4
RG_LRU_C = 8.0
D_CONV = 1024
CONV_WIDTH = 31
D_FF = 2816
FFN_RESIDUAL_WEIGHT = 0.5
EPS = 1e-6
IN_SIZES = (D_RNN, D_RNN, D_CONV, D_CONV, D_MODEL, D_MODEL)
IN_TOTAL = sum(IN_SIZES)
IN_SPLITS = tuple(int(v) for v in np.cumsum(IN_SIZES)[:-1])

kernel_name = "hybrid_rglru_conformer_macaron"


def rmsnorm(x, g):
    xf = x.astype(jnp.float32)
    y = xf * lax.rsqrt(jnp.mean(xf * xf, axis=-1, keepdims=True) + EPS)
    return (y * g.astype(jnp.float32)).astype(x.dtype)


def layernorm(x, g, b):
    xf = x.astype(jnp.float32)
    mu = jnp.mean(xf, axis=-1, keepdims=True)
    var = jnp.mean(jnp.square(xf - mu), axis=-1, keepdims=True)
    y = (xf - mu) * lax.rsqrt(var + EPS)
    return (y * g.astype(jnp.float32) + b.astype(jnp.float32)).astype(x.dtype)


def swiglu_ffn(h, w_gu, w_down):
    gate, up = jnp.split(h @ w_gu, 2, axis=-1)
    return (jax.nn.silu(gate) * up) @ w_down


def causal_depthwise_conv(x, w, b):
    k_width, channels = w.shape
    out = lax.conv_general_dilated(
        x, w[:, None, :].astype(x.dtype), window_strides=(1,), padding=((k_width - 1, 0),),
        dimension_numbers=("NWC", "WIO", "NWC"), feature_group_count=channels)
    return out + b


def rg_lru(x, w_a, b_a, w_x, b_x, lam):
    bsz, t_len, _ = x.shape
    xb = x.reshape(bsz, t_len, N_RNN_HEADS, RNN_HEAD_DIM)
    r = jax.nn.sigmoid((jnp.einsum("bthi,hij->bthj", xb, w_a).reshape(bsz, t_len, D_RNN) + b_a).astype(jnp.float32))
    i = jax.nn.sigmoid((jnp.einsum("bthi,hij->bthj", xb, w_x).reshape(bsz, t_len, D_RNN) + b_x).astype(jnp.float32))
    log_a = -RG_LRU_C * r * jax.nn.softplus(-lam.astype(jnp.float32))
    a = jnp.exp(log_a)
    u = jnp.sqrt(-jnp.expm1(2.0 * log_a)) * (i * x.astype(jnp.float32))

    def combine(left, right):
        a_l, h_l = left
        a_r, h_r = right
        return a_l * a_r, a_r * h_l + h_r

    _, h = lax.associative_scan(combine, (a, u), axis=1)
    return h.astype(x.dtype)


def token_mixer(h, w_in, b_in, rnn_conv_w, rnn_conv_b, rg_w_a, rg_b_a, rg_w_x, rg_b_x, rg_lambda,
                rnn_w_proj, conv_dw_w, conv_dw_b, conv_ln_g, conv_ln_b, conv_w_proj, conv_b_proj, w_out):
    proj = h @ w_in + b_in
    x_rnn, y_rnn, glu_v, glu_g, gate_a, gate_b = jnp.split(proj, IN_SPLITS, axis=-1)
    xr = causal_depthwise_conv(x_rnn, rnn_conv_w, rnn_conv_b)
    xr = rg_lru(xr, rg_w_a, rg_b_a, rg_w_x, rg_b_x, rg_lambda)
    y_a = (xr * jax.nn.gelu(y_rnn)) @ rnn_w_proj
    v = glu_v * jax.nn.sigmoid(glu_g)
    v = causal_depthwise_conv(v, conv_dw_w, conv_dw_b)
    v = jax.nn.silu(layernorm(v, conv_ln_g, conv_ln_b))
    y_b = v @ conv_w_proj + conv_b_proj
    merged = jax.nn.sigmoid(gate_a) * y_a + jax.nn.sigmoid(gate_b) * y_b
    return merged @ w_out


def setup_inputs(seed: int = 0) -> dict:
    key = jax.random.key(seed)
    ks = iter(jax.random.split(key, 40))
    L = DEPTH
    f32 = jnp.float32

    def nrm(shape, fan_in):
        return jax.random.normal(next(ks), shape, f32) * (fan_in ** -0.5)

    def gain(shape):
        return 1.0 + 0.02 * jax.random.normal(next(ks), shape, f32)

    def bias(shape):
        return 0.01 * jax.random.normal(next(ks), shape, f32)

    x = jax.random.normal(next(ks), (BATCH, SEQ, D_MODEL), f32)
    meta_tokens = jax.random.normal(next(ks), (N_META, D_MODEL), f32)
    u = jax.random.uniform(next(ks), (L, D_RNN), f32, minval=0.9, maxval=0.999)
    s = u ** (1.0 / RG_LRU_C)
    rg_lambda = jnp.log(s) - jnp.log1p(-s)
    return {
        "x": x,
        "meta_tokens": meta_tokens,
        "ffn1_norm": gain((L, D_MODEL)),
        "ffn1_w_gu": nrm((L, D_MODEL, 2 * D_FF), D_MODEL),
        "ffn1_w_down": nrm((L, D_FF, D_MODEL), D_FF),
        "mix_norm": gain((L, D_MODEL)),
        "w_in": nrm((L, D_MODEL, IN_TOTAL), D_MODEL),
        "b_in": bias((L, IN_TOTAL)),
        "rnn_conv_w": nrm((L, RNN_CONV_WIDTH, D_RNN), RNN_CONV_WIDTH),
        "rnn_conv_b": bias((L, D_RNN)),
        "rg_w_a": nrm((L, N_RNN_HEADS, RNN_HEAD_DIM, RNN_HEAD_DIM), RNN_HEAD_DIM),
        "rg_b_a": bias((L, D_RNN)),
        "rg_w_x": nrm((L, N_RNN_HEADS, RNN_HEAD_DIM, RNN_HEAD_DIM), RNN_HEAD_DIM),
        "rg_b_x": bias((L, D_RNN)),
        "rg_lambda": rg_lambda,
        "rnn_w_proj": nrm((L, D_RNN, D_MODEL), D_RNN),
        "conv_dw_w": nrm((L, CONV_WIDTH, D_CONV), CONV_WIDTH),
        "conv_dw_b": bias((L, D_CONV)),
        "conv_ln_g": gain((L, D_CONV)),
        "conv_ln_b": bias((L, D_CONV)),
        "conv_w_proj": nrm((L, D_CONV, D_MODEL), D_CONV),
        "conv_b_proj": bias((L, D_MODEL)),
        "w_out": nrm((L, D_MODEL, D_MODEL), D_MODEL),
        "ffn2_norm": gain((L, D_MODEL)),
        "ffn2_w_gu": nrm((L, D_MODEL, 2 * D_FF), D_MODEL),
        "ffn2_w_down": nrm((L, D_FF, D_MODEL), D_FF),
        "final_norm": gain((D_MODEL,)),
    }


def reference(x, meta_tokens, ffn1_norm, ffn1_w_gu, ffn1_w_down, mix_norm, w_in, b_in,
              rnn_conv_w, rnn_conv_b, rg_w_a, rg_b_a, rg_w_x, rg_b_x, rg_lambda, rnn_w_proj,
              conv_dw_w, conv_dw_b, conv_ln_g, conv_ln_b, conv_w_proj, conv_b_proj, w_out,
              ffn2_norm, ffn2_w_gu, ffn2_w_down, final_norm):
    bsz = x.shape[0]
    meta = jnp.broadcast_to(meta_tokens.astype(x.dtype)[None], (bsz, N_META, x.shape[-1]))
    h = jnp.concatenate([meta, x], axis=1)
    for l in range(DEPTH):
        h = h + FFN_RESIDUAL_WEIGHT * swiglu_ffn(rmsnorm(h, ffn1_norm[l]), ffn1_w_gu[l], ffn1_w_down[l])
        h = h + token_mixer(rmsnorm(h, mix_norm[l]), w_in[l], b_in[l], rnn_conv_w[l], rnn_conv_b[l],
                            rg_w_a[l], rg_b_a[l], rg_w_x[l], rg_b_x[l], rg_lambda[l], rnn_w_proj[l],
                            conv_dw_w[l], conv_dw_b[l], conv_ln_g[l], conv_ln_b[l], conv_w_proj[l],
                            conv_b_proj[l], w_out[l])
        h = h + FFN_RESIDUAL_WEIGHT * swiglu_ffn(rmsnorm(h, ffn2_norm[l]), ffn2_w_gu[l], ffn2_w_down[l])
    return rmsnorm(h, final_norm)[:, N_META:, :]
```

```python
import numpy as np
import concourse.bass as bass
import concourse.mybir as mybir
from concourse.bass_utils import run_bass_kernel_spmd

F32 = mybir.dt.float32
BF16 = mybir.dt.bfloat16
AF = mybir.ActivationFunctionType
ALU = mybir.AluOpType

D = 1024
KC = 8
DFF = 2816
JC = 22
NCORES = 8
HALO = 32
TMAIN = 2056
T = TMAIN + HALO
TS = [344] * 5 + [336]
MS = [HALO + 344 * i for i in range(6)]
NT = 6
NMAX = 376
EPS = 1e-6
NSLOT = 10

PC = {}
_o = 0
for _n, _w in [("g1", 8), ("gm", 8), ("g2", 8), ("gf", 8), ("bin", 48), ("w4", 32), ("cb4", 8), ("ba", 8),
               ("bx", 8), ("lam", 8), ("w31", 248), ("cb31", 8), ("lng", 8), ("lnb", 8), ("bp", 8),
               ("flag", 1), ("mask", 8)]:
    PC[_n] = _o
    _o += _w
NP = _o


class Sched:
    ENGS = ("pe", "act", "dve", "pool", "sp")

    def __init__(self):
        self.ops = {e: [] for e in self.ENGS}
        self.cnt = {}
        self.known = {e: {} for e in self.ENGS}
        self.last_w = {}
        self.readers = {}
        self.sem_names = set()
        for e in self.ENGS:
            self.sem_names.add("s_" + e)
            self.cnt["s_" + e] = 0

    def _deps(self, eng, reads, writes):
        deps = {}

        def add(tok, same_ok):
            if tok is None:
                return
            sem, val, teng = tok
            if teng == eng and same_ok and eng == "pe":
                return
            if deps.get(sem, 0) < val:
                deps[sem] = val

        for k in reads:
            add(self.last_w.get(k), False)
        for k in writes:
            add(self.last_w.get(k), True)
            for tok in self.readers.get(k, ()):
                add(tok, True)
        waits = []
        kn = self.known[eng]
        for sem, val in deps.items():
            if kn.get(sem, 0) < val:
                kn[sem] = val
                waits.append((sem, val))
        return waits

    def _commit(self, tok, reads, writes):
        for k in reads:
            self.readers.setdefault(k, []).append(tok)
        for k in writes:
            self.last_w[k] = tok
            self.readers[k] = []

    def op(self, eng, fn, reads=(), writes=()):
        waits = self._deps(eng, reads, writes)
        sem = "s_" + eng
        self.cnt[sem] += 1
        tok = (sem, self.cnt[sem], eng)
        self.ops[eng].append((waits, fn, (sem, 1)))
        self._commit(tok, reads, writes)
        return tok

    def dma(self, eng, fn, sem, reads=(), writes=(), inc=16):
        waits = self._deps(eng, reads, writes)
        if sem not in self.cnt:
            self.cnt[sem] = 0
            self.sem_names.add(sem)
        self.cnt[sem] += inc
        tok = (sem, self.cnt[sem], "dma:" + sem)
        self.ops[eng].append((waits, fn, (sem, inc)))
        self._commit(tok, reads, writes)
        return tok

    def barrier(self):
        for eng in self.ENGS:
            waits = []
            for sem, v in self.cnt.items():
                if v > 0 and self.known[eng].get(sem, 0) < v:
                    self.known[eng][sem] = v
                    waits.append((sem, v))
            self.ops[eng].append((waits, None, None))
        self.last_w.clear()
        self.readers.clear()

    def wait_all(self, eng, sems):
        waits = []
        for sem in sems:
            v = self.cnt.get(sem, 0)
            if v > 0 and self.known[eng].get(sem, 0) < v:
                self.known[eng][sem] = v
                waits.append((sem, v))
        self.ops[eng].append((waits, None, None))

    def finalize(self):
        need = {}
        for e in self.ENGS:
            for waits, fn, inc in self.ops[e]:
                for sem, val in waits:
                    if sem in ("s_" + x for x in self.ENGS):
                        need.setdefault(sem, set()).add(val)
        self.rank = {}
        for sem, vals in need.items():
            self.rank[sem] = {v: i + 1 for i, v in enumerate(sorted(vals))}

    def replay(self, eng, handle, sems):
        mysem = "s_" + eng
        idx = 0
        for waits, fn, inc in self.ops[eng]:
            for sem, val in waits:
                if sem in self.rank:
                    val = self.rank[sem][val]
                handle.wait_ge(sems[sem], val)
            if fn is None:
                continue
            ins = fn(handle)
            if inc is not None and inc[0] == mysem:
                idx += 1
                if idx not in self.rank.get(mysem, {}):
                    continue
            if inc is not None:
                if inc[1] == 1 and inc[0] == "s_cc2":
                    ins.then_inc(sems[inc[0]])
                else:
                    ins.then_inc(sems[inc[0]], inc[1])


def build_program():
    nc = bass.Bass("TRN2", target_bir_lowering=False)

    def din(name, shape, dt=F32):
        return nc.dram_tensor(name, list(shape), dt, kind="ExternalInput").ap()

    xT = din("xT", [D, T])
    params = din("params", [128, NP])
    consts = din("consts", [128, 256])
    wgu = [din("wgu1", [JC, 2, 128, 1024]), din("wgu2", [JC, 2, 128, 1024])]
    wd = [din("wd1", [KC, 128, DFF]), din("wd2", [KC, 128, DFF])]
    win = din("win", [48, 128, 1024])
    wgate = din("wgate", [4, 128, 1024])
    wp = din("wp", [KC, 128, 1024])
    wc = din("wc", [KC, 128, 1024])
    wo = din("wo", [KC, 128, 1024])
    outT = nc.dram_tensor("outT", [D, TMAIN], F32, kind="ExternalOutput").ap()
    win_b = nc.dram_tensor("win_b", [48 * 128, 1024], BF16)
    wgate_b = nc.dram_tensor("wgate_b", [4 * 128, 1024], BF16)
    wp_b = nc.dram_tensor("wp_b", [KC * 128, 1024], BF16)
    wc_b = nc.dram_tensor("wc_b", [KC * 128, 1024], BF16)
    wo_b = nc.dram_tensor("wo_b", [KC * 128, 1024], BF16)
    cc_in = nc.dram_tensor("cc_in", [128, KC], F32)
    cc_out = nc.dram_tensor("cc_out", [NCORES * 128, KC], F32)

    S = Sched()
    from contextlib import ExitStack
    es = ExitStack()

    def sb(name, shape, dt):
        return es.enter_context(nc.sbuf_tensor(name, list(shape), dt))

    with es:
        Bh = sb("Bh", [128, KC, T], F32)
        xn = sb("xn", [128, KC, NMAX], BF16)
        hidb = sb("hidb", [128, 6 * T], BF16)
        fr32 = sb("fr32", [128, 24 * NMAX], F32)
        hid = hidb[:, 0:33 * NMAX].rearrange("p (r n) -> p r n", n=NMAX)
        fr = fr32[:, :].rearrange("p (r n) -> p r n", n=NMAX)
        xna = fr32[:, :].bitcast(BF16)[:, 0:KC * T].rearrange("p (r n) -> p r n", n=T)
        hidg = hidb[:, :].rearrange("p (r n) -> p r n", n=T)
        sq = sb("sq", [128, KC, NMAX], BF16)
        slots = sb("slots", [128, NSLOT, 1024], BF16)
        dg4 = sb("dg4", [128, 32, 128], BF16)
        dg31 = sb("dg31", [128, 2, 31, 128], BF16)
        vrow = sb("vrow", [128, 2, NMAX], BF16)
        prm = sb("prm", [128, NP], F32)
        cst = sb("cst", [128, 256], BF16)
        der = sb("der", [128, 32], F32)
        stat = sb("stat", [128, 4, NMAX], F32)
        tmp = sb("tmp", [128, 2, NMAX], F32)
        state = sb("state", [128, KC], F32)
        gath = sb("gath", [128, NCORES, KC], F32)
        xhalo = sb("xhalo", [128, KC, HALO], BF16)
        vcb = sb("vcb", [128, KC, NMAX], F32)
        ps = [es.enter_context(nc.psum_tensor("ps%d" % b, [128, 512], F32)) for b in range(8)]

        ident = cst[:, 0:128]
        onesm = cst[:, 128:256]

        def pcol(name, i=0):
            return prm[:, PC[name] + i:PC[name] + i + 1]

        slot_ctr = [0]

        def load_w(src_ap, nelem, q="pool", rd=()):
            k = slot_ctr[0] % NSLOT
            slot_ctr[0] += 1
            dst = slots[:, k, 0:nelem]
            S.dma(q, lambda e, d=dst, s=src_ap: e.dma_start(out=d, in_=s), "s_w%d_%s" % (k, q),
                  reads=rd, writes=[("slot", k)])
            return k

        conv_jobs = []
        winf = win.rearrange("a p n -> (a p) n")
        for g in range(48):
            conv_jobs.append((win_b.ap()[g * 128:(g + 1) * 128, :], win[g]))
        for g in range(4):
            conv_jobs.append((wgate_b.ap()[g * 128:(g + 1) * 128, :], wgate[g]))
        for (dst_, src_) in ((wp_b, wp), (wc_b, wc), (wo_b, wo)):
            for g in range(KC):
                conv_jobs.append((dst_.ap()[g * 128:(g + 1) * 128, :], src_[g]))

        def conv_next():
            for _ in range(2):
                if conv_jobs:
                    d, s_ = conv_jobs.pop(0)
                    S.dma("pool", lambda e, d=d, s_=s_: e.dma_start(out=d, in_=s_), "s_cv", reads=(),
                          writes=[("wb",)])

        win_v = win_b.ap().rearrange("(a p) n -> a p n", p=128)
        wgate_v = wgate_b.ap().rearrange("(a p) n -> a p n", p=128)
        wp_v = wp_b.ap().rearrange("(a p) n -> a p n", p=128)
        wc_v = wc_b.ap().rearrange("(a p) n -> a p n", p=128)
        wo_v = wo_b.ap().rearrange("(a p) n -> a p n", p=128)

        def load_m(src_ap):
            return load_w(src_ap, 1024, q="sp", rd=[("wb",)])

        ps_ctr = {"a": 0, "b": 0, "c": 0, "d": 0, "t": 0}
        ps_base = {"a": 0, "b": 2, "c": 4, "d": 6, "t": 0}

        def psb(role):
            b = ps_base[role] + (ps_ctr[role] % 2)
            ps_ctr[role] += 1
            return b

        def mm(bank, n, pairs, reads):
            def fn(e, bank=bank, n=n, pairs=pairs):
                ins = None
                L = len(pairs)
                for i, (l, r) in enumerate(pairs):
                    ins = e.matmul(ps[bank][:, 0:n], l, r, start=(i == 0), stop=(i == L - 1))
                return ins
            S.op("pe", fn, reads=reads, writes=[("ps", bank)])

        def act(out, in_, func, reads, writes, bias=None, scale=None):
            kw = {}
            if bias is not None:
                kw["bias"] = bias
            if scale is not None:
                kw["scale"] = scale
            S.op("act", lambda e: e.activation(out=out, in_=in_, func=func, **kw), reads=reads, writes=writes)

        def dve(fn, reads, writes):
            S.op("dve", fn, reads=reads, writes=writes)

        def hkeys(tiles, chunks=range(KC)):
            return [("h", c, i) for i in tiles for c in chunks]

        def tiles_of(lo, hi):
            res = []
            for i in range(NT):
                s, e = (0 if i == 0 else MS[i]), MS[i] + TS[i]
                if lo < e and hi > s:
                    res.append(i)
            return res

        S.dma("sp", lambda e: e.dma_start(out=prm[:], in_=params), "s_par", writes=[("prm",)])
        S.dma("pool", lambda e: e.dma_start(out=cst[:], in_=consts), "s_cst", writes=[("cst",)])
        xTv = xT.rearrange("(c p) t -> p c t", p=128)
        for i in range(NT):
            s, e_ = (0 if i == 0 else MS[i]), MS[i] + TS[i]
            for c in range(KC):
                S.dma("sp" if c % 2 == 0 else "act", lambda e, s=s, e_=e_, c=c: e.dma_start(out=Bh[:, c, s:e_], in_=xTv[:, c, s:e_]),
                      "s_x%d_%d" % (i, c), writes=hkeys([i], [c]))
        dve(lambda e: e.tensor_scalar(out=der[:, 0:8], in0=prm[:, PC["ba"]:PC["ba"] + 8], scalar1=0.5, scalar2=None,
                                      op0=ALU.mult), [("prm",)], [("der", 0)])
        dve(lambda e: e.tensor_scalar(out=der[:, 8:16], in0=prm[:, PC["bx"]:PC["bx"] + 8], scalar1=0.5, scalar2=None,
                                      op0=ALU.mult), [("prm",)], [("der", 1)])
        act(der[:, 24:32], prm[:, PC["lam"]:PC["lam"] + 8], AF.Exp, [("prm",)], [("der", 3)], scale=-1.0)
        act(der[:, 24:32], der[:, 24:32], AF.Ln, [("der", 3)], [("der", 3)], bias=1.0)
        dve(lambda e: e.tensor_scalar(out=der[:, 16:24], in0=der[:, 24:32], scalar1=-4.0, scalar2=None,
                                      op0=ALU.mult), [("der", 3)], [("der", 2)])
        for c in range(KC):
            for k in range(4):
                dve(lambda e, c=c, k=k: e.tensor_scalar(out=dg4[:, c * 4 + k, :], in0=ident,
                                                        scalar1=pcol("w4", k * 8 + c), scalar2=None, op0=ALU.mult),
                    [("prm",), ("cst",)], [("dg4", c)])
        dve(lambda e: e.memset(state[:], 0.0), [], [("state",)])

        def rmsnorm_to(lo, hi, gname, out_fn, out_keys):
            n = hi - lo
            tl = tiles_of(lo, hi)
            act(sq[:, :, 0:n], Bh[:, :, lo:hi], AF.Square, hkeys(tl), [("sq",)])
            b = psb("c")
            mm(b, n, [(onesm, sq[:, c, 0:n]) for c in range(KC)], [("sq",), ("cst",)])
            act(stat[:, 0, 0:n], ps[b][:, 0:n], AF.Sqrt, [("ps", b)], [("stat", 0)], bias=EPS)
            dve(lambda e: e.reciprocal(out=stat[:, 1, 0:n], in_=stat[:, 0, 0:n]), [("stat", 0)], [("stat", 1)])
            for c in range(KC):
                o = out_fn(c)
                dve(lambda e, c=c, o=o: e.scalar_tensor_tensor(out=o, in0=Bh[:, c, lo:hi], scalar=pcol(gname, c),
                                                               in1=stat[:, 1, 0:n], op0=ALU.mult, op1=ALU.mult),
                    hkeys(tl, [c]) + [("stat", 1), ("prm",)], [out_keys(c)])

        def ffn(which, lo, hi, gname):
            n = hi - lo
            tl = tiles_of(lo, hi)
            rmsnorm_to(lo, hi, gname, lambda c: xn[:, c, 0:n], lambda c: ("xn", c))
            xnk = [("xn", c) for c in range(KC)]
            for j in range(JC):
                kg = load_w(wgu[which][j, 0], 1024)
                ku = load_w(wgu[which][j, 1], 1024)
                ba, bb = psb("a"), psb("b")
                mm(ba, n, [(slots[:, kg, kc * 128:(kc + 1) * 128], xn[:, kc, 0:n]) for kc in range(KC)],
                   xnk + [("slot", kg)])
                mm(bb, n, [(slots[:, ku, kc * 128:(kc + 1) * 128], xn[:, kc, 0:n]) for kc in range(KC)],
                   xnk + [("slot", ku)])
                tj = j % 4
                act(tmp[:, tj, 0:n], ps[ba][:, 0:n], AF.Silu, [("ps", ba)], [("tmp", tj)])
                dve(lambda e, j=j, tj=tj, bb=bb: e.tensor_tensor(out=hid[:, j, 0:n], in0=tmp[:, tj, 0:n],
                                                                 in1=ps[bb][:, 0:n], op=ALU.mult),
                    [("tmp", tj), ("ps", bb)], [("hid", j)])
            for m in range(KC):
                segs = [(0, 8), (8, 16), (16, 22)]
                ks = [load_w(wd[which][m, :, a * 128:b_ * 128], (b_ - a) * 128) for a, b_ in segs]
                b = psb("a")
                pairs, rd = [], []
                for (a, b_), k in zip(segs, ks):
                    rd.append(("slot", k))
                    for kc in range(a, b_):
                        pairs.append((slots[:, k, (kc - a) * 128:(kc - a + 1) * 128], hid[:, kc, 0:n]))
                        rd.append(("hid", kc))
                mm(b, n, pairs, rd)
                dve(lambda e, m=m, b=b: e.scalar_tensor_tensor(out=Bh[:, m, lo:hi], in0=ps[b][:, 0:n], scalar=0.5,
                                                               in1=Bh[:, m, lo:hi], op0=ALU.mult, op1=ALU.add),
                    [("ps", b)] + hkeys(tl, [m]), hkeys(tl, [m]))

        GROUPS = [(0, 6), (6, 12), (12, 17), (17, 22)]

        def ffn_phase(which, gname, first):
            rng = [((0 if (i == 0 and first) else MS[i]), MS[i] + TS[i]) for i in range(NT)]
            normed = set()

            def norm_tile(i):
                if i in normed:
                    return
                normed.add(i)
                lo, hi = rng[i]
                rmsnorm_to(lo, hi, gname, lambda c: xna[:, c, lo:hi], lambda c: ("xna", c, i))

            norm_tile(0)
            for (j0, j1) in GROUPS:
                for j in range(j0, j1):
                    kg = load_w(wgu[which][j, 0], 1024)
                    ku = load_w(wgu[which][j, 1], 1024)
                    if first:
                        conv_next()
                        conv_next()
                    for i, (lo, hi) in enumerate(rng):
                        n = hi - lo
                        norm_tile(i)
                        if i + 1 < NT:
                            norm_tile(i + 1)
                        xk = [("xna", c, i) for c in range(KC)]
                        ba, bb = psb("a"), psb("b")
                        mm(ba, n, [(slots[:, kg, kc * 128:(kc + 1) * 128], xna[:, kc, lo:hi]) for kc in range(KC)],
                           xk + [("slot", kg)])
                        mm(bb, n, [(slots[:, ku, kc * 128:(kc + 1) * 128], xna[:, kc, lo:hi]) for kc in range(KC)],
                           xk + [("slot", ku)])
                        tj = ps_ctr["t"] % 2
                        ps_ctr["t"] += 1
                        act(tmp[:, tj, 0:n], ps[ba][:, 0:n], AF.Silu, [("ps", ba)], [("tmp", tj)])
                        dve(lambda e, j=j, j0=j0, tj=tj, bb=bb, lo=lo, hi=hi, n=n: e.tensor_tensor(
                            out=hidg[:, j - j0, lo:hi], in0=tmp[:, tj, 0:n], in1=ps[bb][:, 0:n], op=ALU.mult),
                            [("tmp", tj), ("ps", bb)], [("hidg", j - j0, i)])
                g = j1 - j0
                for m in range(KC):
                    k = load_w(wd[which][m, :, j0 * 128:j1 * 128], g * 128)
                    for i, (lo, hi) in enumerate(rng):
                        n = hi - lo
                        b = psb("d")
                        mm(b, n, [(slots[:, k, kc * 128:(kc + 1) * 128], hidg[:, kc, lo:hi]) for kc in range(g)],
                           [("slot", k)] + [("hidg", kc, i) for kc in range(g)])
                        dve(lambda e, m=m, b=b, lo=lo, hi=hi, n=n: e.scalar_tensor_tensor(
                            out=Bh[:, m, lo:hi], in0=ps[b][:, 0:n], scalar=0.5, in1=Bh[:, m, lo:hi],
                            op0=ALU.mult, op1=ALU.add),
                            [("ps", b)] + hkeys([i], [m]), hkeys([i], [m]))

        def mixnorm(i, stage2=False):
            lo, hi = MS[i] - HALO, MS[i] + TS[i]
            n = TS[i]
            if stage2 and i > 0:
                dve(lambda e: e.tensor_copy(out=xn[:, :, 0:HALO], in_=xhalo[:]),
                    [("xhalo",)] + [("xn", c) for c in range(KC)], [("xn", c) for c in range(KC)])
                rmsnorm_to(MS[i], hi, "gm", lambda c: xn[:, c, HALO:HALO + n], lambda c: ("xn", c))
            else:
                rmsnorm_to(lo, hi, "gm", lambda c: xn[:, c, 0:hi - lo], lambda c: ("xn", c))
            if stage2:
                dve(lambda e: e.tensor_copy(out=xhalo[:], in_=xn[:, :, n:n + HALO]),
                    [("xn", c) for c in range(KC)], [("xhalo",)])

        xnk = [("xn", c) for c in range(KC)]

        def rnn_front(i, final):
            n = TS[i]
            nx = n + HALO
            for oc in range(KC):
                k = load_m(win_v[oc])
                b = psb("a")
                mm(b, nx, [(slots[:, k, kc * 128:(kc + 1) * 128], xn[:, kc, 0:nx]) for kc in range(KC)],
                   xnk + [("slot", k)])
                act(hid[:, oc, 0:nx], ps[b][:, 0:nx], AF.Identity, [("ps", b), ("prm",)], [("hid", oc)],
                    bias=pcol("bin", oc))
                yield
            if i == 0:
                dve(lambda e: e.tensor_scalar(out=hid[:, 0:8, 0:HALO], in0=hid[:, 0:8, 0:HALO],
                                              scalar1=pcol("flag"), scalar2=None, op0=ALU.mult),
                    [("hid", oc) for oc in range(KC)] + [("prm",)], [("hid", oc) for oc in range(KC)])
            for c in range(KC):
                b = psb("d")
                mm(b, n, [(dg4[:, c * 4 + k, :], hid[:, c, HALO - 3 + k:HALO - 3 + k + n]) for k in range(4)],
                   [("dg4", c), ("hid", c)])
                act(fr[:, 16 + c, 0:n], ps[b][:, 0:n], AF.Identity, [("ps", b), ("prm",)], [("fr", 16 + c)],
                    bias=pcol("cb4", c))
                if c % 2 == 1:
                    dve(lambda e, c=c: e.tensor_copy(out=hid[:, 15 + c:17 + c, 0:n], in_=fr[:, 15 + c:17 + c, 0:n]),
                        [("fr", 15 + c), ("fr", 16 + c)], [("hid", 15 + c), ("hid", 16 + c)])
                    yield
            for hd in range(4):
                k = load_m(wgate_v[hd])
                for g in range(2):
                    for mo in range(2):
                        c = hd * 2 + mo
                        b = psb("b")
                        mm(b, n, [(slots[:, k, g * 512 + kc * 256 + mo * 128:g * 512 + kc * 256 + mo * 128 + 128],
                                   hid[:, 16 + hd * 2 + kc, 0:n]) for kc in range(2)],
                           [("slot", k), ("hid", 16 + hd * 2), ("hid", 17 + hd * 2)])
                        row = g * 8 + c
                        act(fr[:, row, 0:n], ps[b][:, 0:n], AF.Tanh, [("ps", b), ("der", g)], [("fr", row)],
                            bias=der[:, g * 8 + c:g * 8 + c + 1], scale=0.5)
                    yield
            for c in range(KC):
                act(fr[:, c, 0:n], fr[:, c, 0:n], AF.Exp, [("fr", c), ("der", 2)], [("fr", c)],
                    bias=der[:, 16 + c:17 + c], scale=der[:, 16 + c:17 + c])
            yield
            rA = [("fr", c) for c in range(KC)]
            rB = [("fr", 8 + c) for c in range(KC)]
            rC = [("fr", 16 + c) for c in range(KC)]
            dve(lambda e: e.scalar_tensor_tensor(out=fr[:, 8:16, 0:n], in0=fr[:, 8:16, 0:n], scalar=1.0,
                                                 in1=fr[:, 16:24, 0:n], op0=ALU.add, op1=ALU.mult),
                rB + rC, rB)
            dve(lambda e: e.tensor_tensor(out=fr[:, 16:24, 0:n], in0=fr[:, 0:8, 0:n], in1=fr[:, 0:8, 0:n],
                                          op=ALU.mult), rA + rB, rC)
            act(fr[:, 16:24, 0:n], fr[:, 16:24, 0:n], AF.Sqrt, rC, rC, bias=1.0, scale=-1.0)
            yield
            dve(lambda e: e.scalar_tensor_tensor(out=fr[:, 8:16, 0:n], in0=fr[:, 16:24, 0:n], scalar=0.5,
                                                 in1=fr[:, 8:16, 0:n], op0=ALU.mult, op1=ALU.mult),
                rB + rC, rB)
            yield
            if final and i == 0:
                combine_carry()
            for c in range(KC):
                dve(lambda e, c=c: e.tensor_tensor_scan(out=fr[:, 16 + c, 0:n], data0=fr[:, c, 0:n],
                                                        data1=fr[:, 8 + c, 0:n], initial=state[:, c:c + 1],
                                                        op0=ALU.mult, op1=ALU.add),
                    [("fr", c), ("fr", 8 + c), ("state",), ("fr", 16 + c)], [("fr", 16 + c)])
                if c % 4 == 3:
                    yield
            dve(lambda e: e.tensor_copy(out=state[:], in_=fr[:, 16:24, n - 1]), rC + [("state",)], [("state",)])
            if final:
                dve(lambda e: e.tensor_tensor(out=hid[:, 8:16, 0:n], in0=fr[:, 16:24, 0:n], in1=hid[:, 8:16, 0:n],
                                              op=ALU.mult),
                    rC + [("hid", 8 + c) for c in range(KC)], [("hid", 8 + c) for c in range(KC)])
            yield

        def gen_rnn_branch(i):
            n = TS[i]
            nx = n + HALO
            for oc in range(KC):
                k = load_m(win_v[8 + oc])
                b = psb("a")
                mm(b, n, [(slots[:, k, kc * 128:(kc + 1) * 128], xn[:, kc, HALO:nx]) for kc in range(KC)],
                   xnk + [("slot", k)])
                act(hid[:, 8 + oc, 0:n], ps[b][:, 0:n], AF.Gelu_apprx_tanh, [("ps", b), ("prm",)],
                    [("hid", 8 + oc)], bias=pcol("bin", 8 + oc))
                yield
            yield from rnn_front(i, True)
            hgk = [("hid", 8 + c) for c in range(KC)]
            for m in range(KC):
                k1 = load_m(wp_v[m])
                k2 = load_m(win_v[32 + m])
                ba, bb = psb("a"), psb("b")
                mm(ba, n, [(slots[:, k1, kc * 128:(kc + 1) * 128], hid[:, 8 + kc, 0:n]) for kc in range(KC)],
                   hgk + [("slot", k1)])
                mm(bb, n, [(slots[:, k2, kc * 128:(kc + 1) * 128], xn[:, kc, HALO:nx]) for kc in range(KC)],
                   xnk + [("slot", k2)])
                tj = m % 2
                act(tmp[:, tj, 0:n], ps[bb][:, 0:n], AF.Sigmoid, [("ps", bb), ("prm",)], [("tmp", tj)],
                    bias=pcol("bin", 32 + m))
                dve(lambda e, m=m, tj=tj, ba=ba: e.tensor_tensor(out=hid[:, 16 + m, 0:n], in0=tmp[:, tj, 0:n],
                                                                 in1=ps[ba][:, 0:n], op=ALU.mult),
                    [("tmp", tj), ("ps", ba)], [("hid", 16 + m)])
                yield

        def gen_conv_branch(i):
            n = TS[i]
            nx = n + HALO

            def proj(c):
                k1 = load_m(win_v[16 + c])
                k2 = load_m(win_v[24 + c])
                ba, bb = psb("a"), psb("b")
                mm(ba, nx, [(slots[:, k1, kc * 128:(kc + 1) * 128], xn[:, kc, 0:nx]) for kc in range(KC)],
                   xnk + [("slot", k1)])
                mm(bb, nx, [(slots[:, k2, kc * 128:(kc + 1) * 128], xn[:, kc, 0:nx]) for kc in range(KC)],
                   xnk + [("slot", k2)])
                tj = c % 2
                vb = c % 2
                act(stat[:, tj, 0:nx], ps[bb][:, 0:nx], AF.Sigmoid, [("ps", bb), ("prm",)], [("stat", tj)],
                    bias=pcol("bin", 24 + c))
                dve(lambda e: e.scalar_tensor_tensor(
                    out=vrow[:, vb, 0:nx], in0=ps[ba][:, 0:nx], scalar=pcol("bin", 16 + c), in1=stat[:, tj, 0:nx],
                    op0=ALU.add, op1=ALU.mult),
                    [("stat", tj), ("ps", ba), ("prm",)], [("vrow", vb)])
                if i == 0:
                    dve(lambda e: e.tensor_scalar(out=vrow[:, vb, 0:HALO], in0=vrow[:, vb, 0:HALO],
                                                  scalar1=pcol("flag"), scalar2=None, op0=ALU.mult),
                        [("vrow", vb), ("prm",)], [("vrow", vb)])
                dve(lambda e: e.tensor_tensor(
                    out=dg31[:, vb, :, :], in0=ident.unsqueeze(1).to_broadcast([128, 31, 128]),
                    in1=prm[:, PC["w31"]:PC["w31"] + 248].rearrange("p (k c) -> p k c", c=8)[:, :, c]
                    .unsqueeze(2).to_broadcast([128, 31, 128]), op=ALU.mult),
                    [("prm",), ("cst",)], [("dg31", vb)])

            def conv(c):
                vb = c % 2
                b = psb("d")
                mm(b, n, [(dg31[:, vb, k, :], vrow[:, vb, 2 + k:2 + k + n]) for k in range(31)],
                   [("dg31", vb), ("vrow", vb)])
                act(vcb[:, c, 0:n], ps[b][:, 0:n], AF.Identity, [("ps", b), ("prm",)], [("vc", c)],
                    bias=pcol("cb31", c))

            proj(0)
            yield
            for c in range(KC):
                if c + 1 < KC:
                    proj(c + 1)
                    yield
                conv(c)
                yield
            rV = [("vc", c) for c in range(KC)]
            dve(lambda e: e.tensor_copy(out=sq[:, :, 0:n], in_=vcb[:, :, 0:n]), rV, [("sq",)])
            b = psb("c")
            mm(b, n, [(onesm, sq[:, c, 0:n]) for c in range(KC)], [("sq",), ("cst",)])
            yield
            for c in range(KC):
                dve(lambda e, c=c, b=b: e.tensor_tensor(out=vcb[:, c, 0:n], in0=vcb[:, c, 0:n], in1=ps[b][:, 0:n],
                                                        op=ALU.subtract),
                    [("vc", c), ("ps", b)], [("vc", c)])
            yield
            act(sq[:, :, 0:n], vcb[:, :, 0:n], AF.Square, rV, [("sq",)])
            b2 = psb("c")
            mm(b2, n, [(onesm, sq[:, c, 0:n]) for c in range(KC)], [("sq",), ("cst",)])
            yield
            act(stat[:, 2, 0:n], ps[b2][:, 0:n], AF.Sqrt, [("ps", b2)], [("stat", 2)], bias=EPS)
            dve(lambda e: e.reciprocal(out=stat[:, 3, 0:n], in_=stat[:, 2, 0:n]), [("stat", 2)], [("stat", 3)])
            yield
            for c in range(KC):
                dve(lambda e, c=c: e.scalar_tensor_tensor(out=vcb[:, c, 0:n], in0=vcb[:, c, 0:n],
                                                          scalar=pcol("lng", c), in1=stat[:, 3, 0:n],
                                                          op0=ALU.mult, op1=ALU.mult),
                    [("vc", c), ("stat", 3), ("prm",)], [("vc", c)])
                act(hid[:, 24 + c, 0:n], vcb[:, c, 0:n], AF.Silu, [("vc", c), ("prm",)], [("hid", 24 + c)],
                    bias=pcol("lnb", c))
                if c % 2 == 1:
                    yield

        def interleave(*gens):
            gens = list(gens)
            while gens:
                for g in list(gens):
                    try:
                        next(g)
                    except StopIteration:
                        gens.remove(g)

        def mixer_rest(i):
            n = TS[i]
            nx = n + HALO
            lo, hi = MS[i], MS[i] + n
            ga = gen_rnn_branch(i)
            for _ in range(20):
                next(ga)
            interleave(ga, gen_conv_branch(i))
            zk = [("hid", 24 + c) for c in range(KC)]
            for m in range(KC):
                k1 = load_m(wc_v[m])
                k2 = load_m(win_v[40 + m])
                ba, bb = psb("a"), psb("b")
                mm(ba, n, [(slots[:, k1, kc * 128:(kc + 1) * 128], hid[:, 24 + kc, 0:n]) for kc in range(KC)],
                   zk + [("slot", k1)])
                mm(bb, n, [(slots[:, k2, kc * 128:(kc + 1) * 128], xn[:, kc, HALO:nx]) for kc in range(KC)],
                   xnk + [("slot", k2)])
                tj = m % 2
                act(tmp[:, tj, 0:n], ps[bb][:, 0:n], AF.Sigmoid, [("ps", bb), ("prm",)], [("tmp", tj)],
                    bias=pcol("bin", 40 + m))
                dve(lambda e, m=m, tj=tj, ba=ba: e.scalar_tensor_tensor(
                    out=tmp[:, tj, 0:n], in0=ps[ba][:, 0:n], scalar=pcol("bp", m), in1=tmp[:, tj, 0:n],
                    op0=ALU.add, op1=ALU.mult),
                    [("tmp", tj), ("ps", ba), ("prm",)], [("tmp", tj)])
                dve(lambda e, m=m, tj=tj: e.tensor_tensor(out=hid[:, 16 + m, 0:n], in0=tmp[:, tj, 0:n],
                                                          in1=hid[:, 16 + m, 0:n], op=ALU.add),
                    [("tmp", tj), ("hid", 16 + m)], [("hid", 16 + m)])
            mk = [("hid", 16 + c) for c in range(KC)]
            for m in range(KC):
                k = load_m(wo_v[m])
                b = psb("a")
                mm(b, n, [(slots[:, k, kc * 128:(kc + 1) * 128], hid[:, 16 + kc, 0:n]) for kc in range(KC)],
                   mk + [("slot", k)])
                dve(lambda e, m=m, b=b: e.tensor_tensor(out=Bh[:, m, lo:hi], in0=ps[b][:, 0:n], in1=Bh[:, m, lo:hi],
                                                        op=ALU.add),
                    [("ps", b)] + hkeys([i], [m]), hkeys([i], [m]))

        outTv = outT.rearrange("(c p) t -> p c t", p=128)

        def final_out(i):
            n = TS[i]
            lo, hi = MS[i], MS[i] + n
            rmsnorm_to(lo, hi, "gf", lambda c: vcb[:, c, 0:n], lambda c: ("vc", c))
            for c in range(KC):
                S.dma("sp", lambda e, c=c: e.dma_start(out=outTv[:, c, lo - HALO:hi - HALO], in_=vcb[:, c, 0:n]),
                      "s_out%d" % c, reads=[("vc", c)], writes=[("out", i, c)])

        ffn_phase(0, "g1", True)
        S.barrier()
        def adv(g, k):
            for _ in range(k):
                next(g)

        pre_g = {}

        def head_gen(i):
            mixnorm(i)
            yield
            g = rnn_front(i, False)
            pre_g[i] = g
            for _ in range(KC):
                next(g)
                yield

        for _ in head_gen(0):
            pass
        for i in range(NT):
            g = pre_g[i]
            adv(g, 12)
            if i + 1 < NT:
                interleave(g, head_gen(i + 1))
            else:
                for _ in g:
                    pass
        S.dma("pool", lambda e: e.dma_start(out=cc_in.ap(), in_=state[:]), "s_cc1", reads=[("state",)],
              writes=[("cc_in",)])

        def cc_fn(e):
            return e.collective_compute("AllGather", ALU.bypass, replica_groups=[list(range(NCORES))],
                                        ins=[cc_in.ap().opt()], outs=[cc_out.ap().opt()])
        S.dma("pool", cc_fn, "s_cc2", reads=[("cc_in",)], writes=[("cc_out",)], inc=1)
        for r in range(NCORES):
            S.dma("pool", lambda e, r=r: e.dma_start(out=gath[:, r, :], in_=cc_out.ap()[r * 128:(r + 1) * 128, :]),
                  "s_cc3", reads=[("cc_out",)], writes=[("gath",)] if r == NCORES - 1 else [("gath_part", r)])
        def combine_carry():
            dve(lambda e: e.memset(state[:], 0.0), [("state",)], [("state",)])
            for r in range(NCORES):
                dve(lambda e, r=r: e.scalar_tensor_tensor(out=state[:], in0=gath[:, r, :], scalar=pcol("mask", r),
                                                          in1=state[:], op0=ALU.mult, op1=ALU.add),
                    [("gath",), ("state",), ("prm",)], [("state",)])
        for i in range(NT):
            mixnorm(i, True)
            mixer_rest(i)
        S.barrier()
        ffn_phase(1, "g2", False)
        for i in range(NT):
            final_out(i)
        S.wait_all("sp", ["s_out%d" % c for c in range(KC)])

        S.finalize()
        sems = {}
        for name in sorted(S.sem_names):
            sems[name] = es.enter_context(nc.semaphore(name))
        with nc.Block() as block:
            @block.sync
            def _(e):
                S.replay("sp", e, sems)

            @block.gpsimd
            def _(e):
                S.replay("pool", e, sems)

            @block.scalar
            def _(e):
                S.replay("act", e, sems)

            @block.vector
            def _(e):
                S.replay("dve", e, sems)

            @block.tensor
            def _(e):
                S.replay("pe", e, sems)
    return nc


def _vec_cols(v):
    v = np.asarray(v, np.float32).reshape(-1)
    return np.ascontiguousarray(v.reshape(-1, 128).T)


def _kmajor(w):
    K, N = w.shape
    a = w.reshape(K // 128, 128, N // 128, 128)
    return np.ascontiguousarray(a.transpose(2, 1, 0, 3).reshape(N // 128, 128, (K // 128) * 128))


_CACHE = {}


def kernel(x, meta_tokens, ffn1_norm, ffn1_w_gu, ffn1_w_down, mix_norm, w_in, b_in,
           rnn_conv_w, rnn_conv_b, rg_w_a, rg_b_a, rg_w_x, rg_b_x, rg_lambda, rnn_w_proj,
           conv_dw_w, conv_dw_b, conv_ln_g, conv_ln_b, conv_w_proj, conv_b_proj, w_out,
           ffn2_norm, ffn2_w_gu, ffn2_w_down, final_norm):
    f = lambda a: np.asarray(a, np.float32)
    x = f(x)
    B = x.shape[0]
    meta = f(meta_tokens)
    NMETA = meta.shape[0]
    shared = {}
    for nm, wgu_, wd_ in (("1", f(ffn1_w_gu)[0], f(ffn1_w_down)[0]), ("2", f(ffn2_w_gu)[0], f(ffn2_w_down)[0])):
        g = _kmajor(wgu_[:, :DFF])
        u = _kmajor(wgu_[:, DFF:])
        shared["wgu" + nm] = np.ascontiguousarray(np.stack([g, u], axis=1))
        shared["wd" + nm] = _kmajor(wd_)
    shared["win"] = _kmajor(f(w_in)[0])
    wa, wx = f(rg_w_a)[0], f(rg_w_x)[0]
    wg = np.stack([wa, wx], axis=1)
    wg = wg.reshape(4, 2, 2, 128, 256).transpose(0, 3, 1, 2, 4)
    shared["wgate"] = np.ascontiguousarray(wg.reshape(4, 128, 1024))
    shared["wp"] = _kmajor(f(rnn_w_proj)[0])
    shared["wc"] = _kmajor(f(conv_w_proj)[0])
    shared["wo"] = _kmajor(f(w_out)[0])
    consts = np.zeros((128, 256), np.float32)
    consts[:, :128] = np.eye(128, dtype=np.float32)
    consts[:, 128:] = 1.0 / D
    shared["consts"] = consts
    pbase = np.zeros((128, NP), np.float32)

    def put(name, v):
        c = _vec_cols(v)
        pbase[:, PC[name]:PC[name] + c.shape[1]] = c

    put("g1", ffn1_norm); put("gm", mix_norm); put("g2", ffn2_norm); put("gf", final_norm)
    put("bin", b_in)
    put("w4", f(rnn_conv_w)[0].reshape(-1)); put("cb4", rnn_conv_b)
    put("ba", rg_b_a); put("bx", rg_b_x); put("lam", rg_lambda)
    put("w31", f(conv_dw_w)[0].reshape(-1)); put("cb31", conv_dw_b)
    put("lng", conv_ln_g); put("lnb", conv_ln_b); put("bp", conv_b_proj)

    in_maps = []
    nA = TMAIN - NMETA
    for core in range(NCORES):
        b, half = core // 2, core % 2
        tok = np.zeros((T, D), np.float32)
        p = pbase.copy()
        if half == 0:
            tok[HALO:HALO + NMETA] = meta
            tok[HALO + NMETA:] = x[b, :nA]
            p[:, PC["flag"]] = 0.0
        else:
            tok[:] = x[b, nA - HALO:]
            p[:, PC["flag"]] = 1.0
            p[:, PC["mask"] + core - 1] = 1.0
        m = dict(shared)
        m["xT"] = np.ascontiguousarray(tok.T)
        m["params"] = p
        in_maps.append(m)

    if "nc" not in _CACHE:
        _CACHE["nc"] = build_program()
    res = run_bass_kernel_spmd(_CACHE["nc"], in_maps, core_ids=list(range(NCORES)))
    out = np.empty((B, x.shape[1], D), np.float32)
    for core in range(NCORES):
        b, half = core // 2, core % 2
        o = np.asarray(res.results[core]["outT"]).T
        if half == 0:
            out[b, :nA] = o[NMETA:]
        else:
            out[b, nA:] = o
    return out
```

```python
import numpy as np
import concourse.bass as bass
import concourse.mybir as mybir
from concourse.bass_utils import run_bass_kernel_spmd

F32 = mybir.dt.float32
BF16 = mybir.dt.bfloat16
AF = mybir.ActivationFunctionType
ALU = mybir.AluOpType

D = 1024
KC = 8
DFF = 2816
JC = 22
NCORES = 8
HALO = 32
TMAIN = 2056
T = TMAIN + HALO
TS = [344] * 5 + [336]
MS = [HALO + 344 * i for i in range(6)]
NT = 6
NMAX = 376
EPS = 1e-6
NSLOT = 8

PC = {}
_o = 0
for _n, _w in [("g1", 8), ("gm", 8), ("g2", 8), ("gf", 8), ("bin", 48), ("w4", 32), ("cb4", 8), ("ba", 8),
               ("bx", 8), ("lam", 8), ("w31", 248), ("cb31", 8), ("lng", 8), ("lnb", 8), ("bp", 8),
               ("flag", 1), ("mask", 8)]:
    PC[_n] = _o
    _o += _w
NP = _o


class Sched:
    ENGS = ("pe", "act", "dve", "pool", "sp")

    def __init__(self):
        self.ops = {e: [] for e in self.ENGS}
        self.cnt = {}
        self.known = {e: {} for e in self.ENGS}
        self.last_w = {}
        self.readers = {}
        self.sem_names = set()
        for e in self.ENGS:
            self.sem_names.add("s_" + e)
            self.cnt["s_" + e] = 0

    def _deps(self, eng, reads, writes):
        deps = {}

        def add(tok, same_ok):
            if tok is None:
                return
            sem, val, teng = tok
            if teng == eng and same_ok and eng == "pe":
                return
            if deps.get(sem, 0) < val:
                deps[sem] = val

        for k in reads:
            add(self.last_w.get(k), False)
        for k in writes:
            add(self.last_w.get(k), True)
            for tok in self.readers.get(k, ()):
                add(tok, True)
        waits = []
        kn = self.known[eng]
        for sem, val in deps.items():
            if kn.get(sem, 0) < val:
                kn[sem] = val
                waits.append((sem, val))
        return waits

    def _commit(self, tok, reads, writes):
        for k in reads:
            self.readers.setdefault(k, []).append(tok)
        for k in writes:
            self.last_w[k] = tok
            self.readers[k] = []

    def op(self, eng, fn, reads=(), writes=()):
        waits = self._deps(eng, reads, writes)
        sem = "s_" + eng
        self.cnt[sem] += 1
        tok = (sem, self.cnt[sem], eng)
        self.ops[eng].append((waits, fn, (sem, 1)))
        self._commit(tok, reads, writes)
        return tok

    def dma(self, eng, fn, sem, reads=(), writes=(), inc=16):
        waits = self._deps(eng, reads, writes)
        if sem not in self.cnt:
            self.cnt[sem] = 0
            self.sem_names.add(sem)
        self.cnt[sem] += inc
        tok = (sem, self.cnt[sem], "dma:" + sem)
        self.ops[eng].append((waits, fn, (sem, inc)))
        self._commit(tok, reads, writes)
        return tok

    def barrier(self):
        for eng in self.ENGS:
            waits = []
            for sem, v in self.cnt.items():
                if v > 0 and self.known[eng].get(sem, 0) < v:
                    self.known[eng][sem] = v
                    waits.append((sem, v))
            self.ops[eng].append((waits, None, None))
        self.last_w.clear()
        self.readers.clear()

    def wait_all(self, eng, sems):
        waits = []
        for sem in sems:
            v = self.cnt.get(sem, 0)
            if v > 0 and self.known[eng].get(sem, 0) < v:
                self.known[eng][sem] = v
                waits.append((sem, v))
        self.ops[eng].append((waits, None, None))

    def finalize(self):
        need = {}
        for e in self.ENGS:
            for waits, fn, inc in self.ops[e]:
                for sem, val in waits:
                    if sem in ("s_" + x for x in self.ENGS):
                        need.setdefault(sem, set()).add(val)
        self.rank = {}
        for sem, vals in need.items():
            self.rank[sem] = {v: i + 1 for i, v in enumerate(sorted(vals))}

    def replay(self, eng, handle, sems):
        mysem = "s_" + eng
        idx = 0
        for waits, fn, inc in self.ops[eng]:
            for sem, val in waits:
                if sem in self.rank:
                    val = self.rank[sem][val]
                handle.wait_ge(sems[sem], val)
            if fn is None:
                continue
            ins = fn(handle)
            if inc is not None and inc[0] == mysem:
                idx += 1
                if idx not in self.rank.get(mysem, {}):
                    continue
            if inc is not None:
                if inc[1] == 1 and inc[0] == "s_cc2":
                    ins.then_inc(sems[inc[0]])
                else:
                    ins.then_inc(sems[inc[0]], inc[1])


def build_program():
    nc = bass.Bass("TRN2", target_bir_lowering=False)

    def din(name, shape, dt=F32):
        return nc.dram_tensor(name, list(shape), dt, kind="ExternalInput").ap()

    xT = din("xT", [D, T])
    params = din("params", [128, NP])
    consts = din("consts", [128, 256])
    wgu = [din("wgu1", [JC, 2, 128, 1024]), din("wgu2", [JC, 2, 128, 1024])]
    wd = [din("wd1", [KC, 128, DFF]), din("wd2", [KC, 128, DFF])]
    win = din("win", [48, 128, 1024])
    wgate = din("wgate", [4, 128, 1024])
    wp = din("wp", [KC, 128, 1024])
    wc = din("wc", [KC, 128, 1024])
    wo = din("wo", [KC, 128, 1024])
    outT = nc.dram_tensor("outT", [D, TMAIN], F32, kind="ExternalOutput").ap()
    win_b = nc.dram_tensor("win_b", [48 * 128, 1024], BF16)
    wgate_b = nc.dram_tensor("wgate_b", [4 * 128, 1024], BF16)
    wp_b = nc.dram_tensor("wp_b", [KC * 128, 1024], BF16)
    wc_b = nc.dram_tensor("wc_b", [KC * 128, 1024], BF16)
    wo_b = nc.dram_tensor("wo_b", [KC * 128, 1024], BF16)
    cc_in = nc.dram_tensor("cc_in", [128, KC], F32)
    cc_out = nc.dram_tensor("cc_out", [NCORES * 128, KC], F32)

    S = Sched()
    from contextlib import ExitStack
    es = ExitStack()

    def sb(name, shape, dt):
        return es.enter_context(nc.sbuf_tensor(name, list(shape), dt))

    with es:
        Bh = sb("Bh", [128, KC, T], F32)
        xn = sb("xn", [128, KC, NMAX], BF16)
        hidb = sb("hidb", [128, 6 * T], BF16)
        fr32 = sb("fr32", [128, 24 * NMAX], F32)
        hid = hidb[:, 0:33 * NMAX].rearrange("p (r n) -> p r n", n=NMAX)
        fr = fr32[:, :].rearrange("p (r n) -> p r n", n=NMAX)
        xna = fr32[:, :].bitcast(BF16)[:, 0:KC * T].rearrange("p (r n) -> p r n", n=T)
        hidg = hidb[:, :].rearrange("p (r n) -> p r n", n=T)
        sq = sb("sq", [128, KC, NMAX], BF16)
        slots = sb("slots", [128, NSLOT, 1024], BF16)
        dg4 = sb("dg4", [128, 32, 128], BF16)
        dg31 = sb("dg31", [128, 2, 31, 128], BF16)
        vrow = sb("vrow", [128, 2, NMAX], BF16)
        prm = sb("prm", [128, NP], F32)
        cst = sb("cst", [128, 256], BF16)
        der = sb("der", [128, 32], F32)
        stat = sb("stat", [128, 4, NMAX], F32)
        tmp = sb("tmp", [128, 4, NMAX], F32)
        state = sb("state", [128, KC], F32)
        gath = sb("gath", [128, NCORES, KC], F32)
        xhalo = sb("xhalo", [128, KC, HALO], BF16)
        vcb = sb("vcb", [128, KC, NMAX], F32)
        ps = [es.enter_context(nc.psum_tensor("ps%d" % b, [128, 512], F32)) for b in range(8)]

        ident = cst[:, 0:128]
        onesm = cst[:, 128:256]

        def pcol(name, i=0):
            return prm[:, PC[name] + i:PC[name] + i + 1]

        slot_ctr = [0]

        def load_w(src_ap, nelem, q="pool", rd=()):
            k = slot_ctr[0] % NSLOT
            slot_ctr[0] += 1
            dst = slots[:, k, 0:nelem]
            S.dma(q, lambda e, d=dst, s=src_ap: e.dma_start(out=d, in_=s), "s_w%d_%s" % (k, q),
                  reads=rd, writes=[("slot", k)])
            return k

        conv_jobs = []
        winf = win.rearrange("a p n -> (a p) n")
        for g in range(48):
            conv_jobs.append((win_b.ap()[g * 128:(g + 1) * 128, :], win[g]))
        for g in range(4):
            conv_jobs.append((wgate_b.ap()[g * 128:(g + 1) * 128, :], wgate[g]))
        for (dst_, src_) in ((wp_b, wp), (wc_b, wc), (wo_b, wo)):
            for g in range(KC):
                conv_jobs.append((dst_.ap()[g * 128:(g + 1) * 128, :], src_[g]))

        def conv_next():
            for _ in range(2):
                if conv_jobs:
                    d, s_ = conv_jobs.pop(0)
                    S.dma("pool", lambda e, d=d, s_=s_: e.dma_start(out=d, in_=s_), "s_cv", reads=(),
                          writes=[("wb",)])

        win_v = win_b.ap().rearrange("(a p) n -> a p n", p=128)
        wgate_v = wgate_b.ap().rearrange("(a p) n -> a p n", p=128)
        wp_v = wp_b.ap().rearrange("(a p) n -> a p n", p=128)
        wc_v = wc_b.ap().rearrange("(a p) n -> a p n", p=128)
        wo_v = wo_b.ap().rearrange("(a p) n -> a p n", p=128)

        def load_m(src_ap):
            return load_w(src_ap, 1024, q="sp", rd=[("wb",)])

        ps_ctr = {"a": 0, "b": 0, "c": 0, "d": 0, "t": 0, "e": 0, "f": 0}
        ps_base = {"a": 0, "b": 2, "c": 4, "d": 6, "t": 0, "e": 6, "f": 7}
        ps_mod = {"e": 1, "f": 1}

        def psb(role):
            b = ps_base[role] + (ps_ctr[role] % ps_mod.get(role, 2))
            ps_ctr[role] += 1
            return b

        def mm(bank, n, pairs, reads):
            def fn(e, bank=bank, n=n, pairs=pairs):
                ins = None
                L = len(pairs)
                for i, (l, r) in enumerate(pairs):
                    ins = e.matmul(ps[bank][:, 0:n], l, r, start=(i == 0), stop=(i == L - 1))
                return ins
            S.op("pe", fn, reads=reads, writes=[("ps", bank)])

        def act(out, in_, func, reads, writes, bias=None, scale=None):
            kw = {}
            if bias is not None:
                kw["bias"] = bias
            if scale is not None:
                kw["scale"] = scale
            S.op("act", lambda e: e.activation(out=out, in_=in_, func=func, **kw), reads=reads, writes=writes)

        def dve(fn, reads, writes):
            S.op("dve", fn, reads=reads, writes=writes)

        def hkeys(tiles, chunks=range(KC)):
            return [("h", c, i) for i in tiles for c in chunks]

        def tiles_of(lo, hi):
            res = []
            for i in range(NT):
                s, e = (0 if i == 0 else MS[i]), MS[i] + TS[i]
                if lo < e and hi > s:
                    res.append(i)
            return res

        S.dma("sp", lambda e: e.dma_start(out=prm[:], in_=params), "s_par", writes=[("prm",)])
        S.dma("pool", lambda e: e.dma_start(out=cst[:], in_=consts), "s_cst", writes=[("cst",)])
        xTv = xT.rearrange("(c p) t -> p c t", p=128)
        for i in range(NT):
            s, e_ = (0 if i == 0 else MS[i]), MS[i] + TS[i]
            for c in range(KC):
                S.dma("sp" if c % 2 == 0 else "act", lambda e, s=s, e_=e_, c=c: e.dma_start(out=Bh[:, c, s:e_], in_=xTv[:, c, s:e_]),
                      "s_x%d_%d" % (i, c), writes=hkeys([i], [c]))
        dve(lambda e: e.tensor_scalar(out=der[:, 0:8], in0=prm[:, PC["ba"]:PC["ba"] + 8], scalar1=0.5, scalar2=None,
                                      op0=ALU.mult), [("prm",)], [("der", 0)])
        dve(lambda e: e.tensor_scalar(out=der[:, 8:16], in0=prm[:, PC["bx"]:PC["bx"] + 8], scalar1=0.5, scalar2=None,
                                      op0=ALU.mult), [("prm",)], [("der", 1)])
        act(der[:, 24:32], prm[:, PC["lam"]:PC["lam"] + 8], AF.Exp, [("prm",)], [("der", 3)], scale=-1.0)
        act(der[:, 24:32], der[:, 24:32], AF.Ln, [("der", 3)], [("der", 3)], bias=1.0)
        dve(lambda e: e.tensor_scalar(out=der[:, 16:24], in0=der[:, 24:32], scalar1=-4.0, scalar2=None,
                                      op0=ALU.mult), [("der", 3)], [("der", 2)])
        for c in range(KC):
            for k in range(4):
                dve(lambda e, c=c, k=k: e.tensor_scalar(out=dg4[:, c * 4 + k, :], in0=ident,
                                                        scalar1=pcol("w4", k * 8 + c), scalar2=None, op0=ALU.mult),
                    [("prm",), ("cst",)], [("dg4", c)])
        dve(lambda e: e.memset(state[:], 0.0), [], [("state",)])

        def rmsnorm_to(lo, hi, gname, out_fn, out_keys):
            n = hi - lo
            tl = tiles_of(lo, hi)
            act(sq[:, :, 0:n], Bh[:, :, lo:hi], AF.Square, hkeys(tl), [("sq",)])
            b = psb("c")
            mm(b, n, [(onesm, sq[:, c, 0:n]) for c in range(KC)], [("sq",), ("cst",)])
            act(stat[:, 0, 0:n], ps[b][:, 0:n], AF.Sqrt, [("ps", b)], [("stat", 0)], bias=EPS)
            dve(lambda e: e.reciprocal(out=stat[:, 1, 0:n], in_=stat[:, 0, 0:n]), [("stat", 0)], [("stat", 1)])
            for c in range(KC):
                o = out_fn(c)
                dve(lambda e, c=c, o=o: e.scalar_tensor_tensor(out=o, in0=Bh[:, c, lo:hi], scalar=pcol(gname, c),
                                                               in1=stat[:, 1, 0:n], op0=ALU.mult, op1=ALU.mult),
                    hkeys(tl, [c]) + [("stat", 1), ("prm",)], [out_keys(c)])

        def ffn(which, lo, hi, gname):
            n = hi - lo
            tl = tiles_of(lo, hi)
            rmsnorm_to(lo, hi, gname, lambda c: xn[:, c, 0:n], lambda c: ("xn", c))
            xnk = [("xn", c) for c in range(KC)]
            for j in range(JC):
                kg = load_w(wgu[which][j, 0], 1024)
                ku = load_w(wgu[which][j, 1], 1024)
                ba, bb = psb("a"), psb("b")
                mm(ba, n, [(slots[:, kg, kc * 128:(kc + 1) * 128], xn[:, kc, 0:n]) for kc in range(KC)],
                   xnk + [("slot", kg)])
                mm(bb, n, [(slots[:, ku, kc * 128:(kc + 1) * 128], xn[:, kc, 0:n]) for kc in range(KC)],
                   xnk + [("slot", ku)])
                tj = j % 4
                act(tmp[:, tj, 0:n], ps[ba][:, 0:n], AF.Silu, [("ps", ba)], [("tmp", tj)])
                dve(lambda e, j=j, tj=tj, bb=bb: e.tensor_tensor(out=hid[:, j, 0:n], in0=tmp[:, tj, 0:n],
                                                                 in1=ps[bb][:, 0:n], op=ALU.mult),
                    [("tmp", tj), ("ps", bb)], [("hid", j)])
            for m in range(KC):
                segs = [(0, 8), (8, 16), (16, 22)]
                ks = [load_w(wd[which][m, :, a * 128:b_ * 128], (b_ - a) * 128) for a, b_ in segs]
                b = psb("a")
                pairs, rd = [], []
                for (a, b_), k in zip(segs, ks):
                    rd.append(("slot", k))
                    for kc in range(a, b_):
                        pairs.append((slots[:, k, (kc - a) * 128:(kc - a + 1) * 128], hid[:, kc, 0:n]))
                        rd.append(("hid", kc))
                mm(b, n, pairs, rd)
                dve(lambda e, m=m, b=b: e.scalar_tensor_tensor(out=Bh[:, m, lo:hi], in0=ps[b][:, 0:n], scalar=0.5,
                                                               in1=Bh[:, m, lo:hi], op0=ALU.mult, op1=ALU.add),
                    [("ps", b)] + hkeys(tl, [m]), hkeys(tl, [m]))

        GROUPS = [(0, 6), (6, 12), (12, 17), (17, 22)]

        def ffn_phase(which, gname, first):
            rng = [((0 if (i == 0 and first) else MS[i]), MS[i] + TS[i]) for i in range(NT)]
            normed = set()

            def norm_tile(i):
                if i in normed:
                    return
                normed.add(i)
                lo, hi = rng[i]
                rmsnorm_to(lo, hi, gname, lambda c: xna[:, c, lo:hi], lambda c: ("xna", c, i))

            norm_tile(0)
            for (j0, j1) in GROUPS:
                for j in range(j0, j1):
                    kg = load_w(wgu[which][j, 0], 1024)
                    ku = load_w(wgu[which][j, 1], 1024)
                    if first:
                        conv_next()
                        conv_next()
                    for i, (lo, hi) in enumerate(rng):
                        n = hi - lo
                        norm_tile(i)
                        if i + 1 < NT:
                            norm_tile(i + 1)
                        xk = [("xna", c, i) for c in range(KC)]
                        ba, bb = psb("a"), psb("b")
                        mm(ba, n, [(slots[:, kg, kc * 128:(kc + 1) * 128], xna[:, kc, lo:hi]) for kc in range(KC)],
                           xk + [("slot", kg)])
                        mm(bb, n, [(slots[:, ku, kc * 128:(kc + 1) * 128], xna[:, kc, lo:hi]) for kc in range(KC)],
                           xk + [("slot", ku)])
                        tj = ps_ctr["t"] % 4
                        ps_ctr["t"] += 1
                        act(tmp[:, tj, 0:n], ps[ba][:, 0:n], AF.Silu, [("ps", ba)], [("tmp", tj)])
                        dve(lambda e, j=j, j0=j0, tj=tj, bb=bb, lo=lo, hi=hi, n=n: e.tensor_tensor(
                            out=hidg[:, j - j0, lo:hi], in0=tmp[:, tj, 0:n], in1=ps[bb][:, 0:n], op=ALU.mult),
                            [("tmp", tj), ("ps", bb)], [("hidg", j - j0, i)])
                g = j1 - j0
                for m in range(KC):
                    k = load_w(wd[which][m, :, j0 * 128:j1 * 128], g * 128)
                    for i, (lo, hi) in enumerate(rng):
                        n = hi - lo
                        b = psb("d")
                        mm(b, n, [(slots[:, k, kc * 128:(kc + 1) * 128], hidg[:, kc, lo:hi]) for kc in range(g)],
                           [("slot", k)] + [("hidg", kc, i) for kc in range(g)])
                        dve(lambda e, m=m, b=b, lo=lo, hi=hi, n=n: e.scalar_tensor_tensor(
                            out=Bh[:, m, lo:hi], in0=ps[b][:, 0:n], scalar=0.5, in1=Bh[:, m, lo:hi],
                            op0=ALU.mult, op1=ALU.add),
                            [("ps", b)] + hkeys([i], [m]), hkeys([i], [m]))

        def mixnorm(i, stage2=False):
            lo, hi = MS[i] - HALO, MS[i] + TS[i]
            n = TS[i]
            if stage2 and i > 0:
                dve(lambda e: e.tensor_copy(out=xn[:, :, 0:HALO], in_=xhalo[:]),
                    [("xhalo",)] + [("xn", c) for c in range(KC)], [("xn", c) for c in range(KC)])
                rmsnorm_to(MS[i], hi, "gm", lambda c: xn[:, c, HALO:HALO + n], lambda c: ("xn", c))
            else:
                rmsnorm_to(lo, hi, "gm", lambda c: xn[:, c, 0:hi - lo], lambda c: ("xn", c))
            if stage2:
                dve(lambda e: e.tensor_copy(out=xhalo[:], in_=xn[:, :, n:n + HALO]),
                    [("xn", c) for c in range(KC)], [("xhalo",)])

        xnk = [("xn", c) for c in range(KC)]

        def rnn_front(i, final):
            n = TS[i]
            nx = n + HALO
            for oc in range(KC):
                k = load_m(win_v[oc])
                b = psb("a")
                mm(b, nx, [(slots[:, k, kc * 128:(kc + 1) * 128], xn[:, kc, 0:nx]) for kc in range(KC)],
                   xnk + [("slot", k)])
                act(hid[:, oc, 0:nx], ps[b][:, 0:nx], AF.Identity, [("ps", b), ("prm",)], [("hid", oc)],
                    bias=pcol("bin", oc))
                yield
            if i == 0:
                dve(lambda e: e.tensor_scalar(out=hid[:, 0:8, 0:HALO], in0=hid[:, 0:8, 0:HALO],
                                              scalar1=pcol("flag"), scalar2=None, op0=ALU.mult),
                    [("hid", oc) for oc in range(KC)] + [("prm",)], [("hid", oc) for oc in range(KC)])
            for c in range(KC):
                b = psb("b")
                mm(b, n, [(dg4[:, c * 4 + k, :], hid[:, c, HALO - 3 + k:HALO - 3 + k + n]) for k in range(4)],
                   [("dg4", c), ("hid", c)])
                act(fr[:, 16 + c, 0:n], ps[b][:, 0:n], AF.Identity, [("ps", b), ("prm",)], [("fr", 16 + c)],
                    bias=pcol("cb4", c))
                if c % 2 == 1:
                    dve(lambda e, c=c: e.tensor_copy(out=hid[:, 15 + c:17 + c, 0:n], in_=fr[:, 15 + c:17 + c, 0:n]),
                        [("fr", 15 + c), ("fr", 16 + c)], [("hid", 15 + c), ("hid", 16 + c)])
                    yield
            for hd in range(4):
                k = load_m(wgate_v[hd])
                for g in range(2):
                    for mo in range(2):
                        c = hd * 2 + mo
                        b = psb("b")
                        mm(b, n, [(slots[:, k, g * 512 + kc * 256 + mo * 128:g * 512 + kc * 256 + mo * 128 + 128],
                                   hid[:, 16 + hd * 2 + kc, 0:n]) for kc in range(2)],
                           [("slot", k), ("hid", 16 + hd * 2), ("hid", 17 + hd * 2)])
                        row = g * 8 + c
                        act(fr[:, row, 0:n], ps[b][:, 0:n], AF.Tanh, [("ps", b), ("der", g)], [("fr", row)],
                            bias=der[:, g * 8 + c:g * 8 + c + 1], scale=0.5)
                    yield
            for c in range(KC):
                act(fr[:, c, 0:n], fr[:, c, 0:n], AF.Exp, [("fr", c), ("der", 2)], [("fr", c)],
                    bias=der[:, 16 + c:17 + c], scale=der[:, 16 + c:17 + c])
            yield
            rA = [("fr", c) for c in range(KC)]
            rB = [("fr", 8 + c) for c in range(KC)]
            rC = [("fr", 16 + c) for c in range(KC)]
            dve(lambda e: e.scalar_tensor_tensor(out=fr[:, 8:16, 0:n], in0=fr[:, 8:16, 0:n], scalar=1.0,
                                                 in1=fr[:, 16:24, 0:n], op0=ALU.add, op1=ALU.mult),
                rB + rC, rB)
            dve(lambda e: e.tensor_tensor(out=fr[:, 16:24, 0:n], in0=fr[:, 0:8, 0:n], in1=fr[:, 0:8, 0:n],
                                          op=ALU.mult), rA + rB, rC)
            act(fr[:, 16:24, 0:n], fr[:, 16:24, 0:n], AF.Sqrt, rC, rC, bias=1.0, scale=-1.0)
            yield
            dve(lambda e: e.scalar_tensor_tensor(out=fr[:, 8:16, 0:n], in0=fr[:, 16:24, 0:n], scalar=0.5,
                                                 in1=fr[:, 8:16, 0:n], op0=ALU.mult, op1=ALU.mult),
                rB + rC, rB)
            yield
            if final and i == 0:
                combine_carry()
            for c in range(KC):
                dve(lambda e, c=c: e.tensor_tensor_scan(out=fr[:, 16 + c, 0:n], data0=fr[:, c, 0:n],
                                                        data1=fr[:, 8 + c, 0:n], initial=state[:, c:c + 1],
                                                        op0=ALU.mult, op1=ALU.add),
                    [("fr", c), ("fr", 8 + c), ("state",), ("fr", 16 + c)], [("fr", 16 + c)])
                if c % 4 == 3:
                    yield
            dve(lambda e: e.tensor_copy(out=state[:], in_=fr[:, 16:24, n - 1]), rC + [("state",)], [("state",)])
            if final:
                dve(lambda e: e.tensor_tensor(out=hid[:, 8:16, 0:n], in0=fr[:, 16:24, 0:n], in1=hid[:, 8:16, 0:n],
                                              op=ALU.mult),
                    rC + [("hid", 8 + c) for c in range(KC)], [("hid", 8 + c) for c in range(KC)])
            yield

        def gen_rnn_branch(i):
            n = TS[i]
            nx = n + HALO
            for oc in range(KC):
                k = load_m(win_v[8 + oc])
                b = psb("a")
                mm(b, n, [(slots[:, k, kc * 128:(kc + 1) * 128], xn[:, kc, HALO:nx]) for kc in range(KC)],
                   xnk + [("slot", k)])
                act(hid[:, 8 + oc, 0:n], ps[b][:, 0:n], AF.Gelu_apprx_tanh, [("ps", b), ("prm",)],
                    [("hid", 8 + oc)], bias=pcol("bin", 8 + oc))
                yield
            yield from rnn_front(i, True)
            hgk = [("hid", 8 + c) for c in range(KC)]
            for m in range(KC):
                k1 = load_m(wp_v[m])
                k2 = load_m(win_v[32 + m])
                ba, bb = psb("a"), psb("b")
                mm(ba, n, [(slots[:, k1, kc * 128:(kc + 1) * 128], hid[:, 8 + kc, 0:n]) for kc in range(KC)],
                   hgk + [("slot", k1)])
                mm(bb, n, [(slots[:, k2, kc * 128:(kc + 1) * 128], xn[:, kc, HALO:nx]) for kc in range(KC)],
                   xnk + [("slot", k2)])
                tj = m % 4
                act(tmp[:, tj, 0:n], ps[bb][:, 0:n], AF.Sigmoid, [("ps", bb), ("prm",)], [("tmp", tj)],
                    bias=pcol("bin", 32 + m))
                dve(lambda e, m=m, tj=tj, ba=ba: e.tensor_tensor(out=hid[:, 16 + m, 0:n], in0=tmp[:, tj, 0:n],
                                                                 in1=ps[ba][:, 0:n], op=ALU.mult),
                    [("tmp", tj), ("ps", ba)], [("hid", 16 + m)])
                yield

        def gen_conv_branch(i):
            n = TS[i]
            nx = n + HALO

            def proj(c):
                k1 = load_m(win_v[16 + c])
                k2 = load_m(win_v[24 + c])
                ba, bb = psb("c"), psb("e")
                mm(ba, nx, [(slots[:, k1, kc * 128:(kc + 1) * 128], xn[:, kc, 0:nx]) for kc in range(KC)],
                   xnk + [("slot", k1)])
                mm(bb, nx, [(slots[:, k2, kc * 128:(kc + 1) * 128], xn[:, kc, 0:nx]) for kc in range(KC)],
                   xnk + [("slot", k2)])
                tj = c % 2
                vb = c % 2
                act(stat[:, tj, 0:nx], ps[bb][:, 0:nx], AF.Sigmoid, [("ps", bb), ("prm",)], [("stat", tj)],
                    bias=pcol("bin", 24 + c))
                dve(lambda e: e.scalar_tensor_tensor(
                    out=vrow[:, vb, 0:nx], in0=ps[ba][:, 0:nx], scalar=pcol("bin", 16 + c), in1=stat[:, tj, 0:nx],
                    op0=ALU.add, op1=ALU.mult),
                    [("stat", tj), ("ps", ba), ("prm",)], [("vrow", vb)])
                if i == 0:
                    dve(lambda e: e.tensor_scalar(out=vrow[:, vb, 0:HALO], in0=vrow[:, vb, 0:HALO],
                                                  scalar1=pcol("flag"), scalar2=None, op0=ALU.mult),
                        [("vrow", vb), ("prm",)], [("vrow", vb)])
                dve(lambda e: e.tensor_tensor(
                    out=dg31[:, vb, :, :], in0=ident.unsqueeze(1).to_broadcast([128, 31, 128]),
                    in1=prm[:, PC["w31"]:PC["w31"] + 248].rearrange("p (k c) -> p k c", c=8)[:, :, c]
                    .unsqueeze(2).to_broadcast([128, 31, 128]), op=ALU.mult),
                    [("prm",), ("cst",)], [("dg31", vb)])

            def conv(c):
                vb = c % 2
                b = psb("f")
                mm(b, n, [(dg31[:, vb, k, :], vrow[:, vb, 2 + k:2 + k + n]) for k in range(31)],
                   [("dg31", vb), ("vrow", vb)])
                act(vcb[:, c, 0:n], ps[b][:, 0:n], AF.Identity, [("ps", b), ("prm",)], [("vc", c)],
                    bias=pcol("cb31", c))

            proj(0)
            yield
            for c in range(KC):
                if c + 1 < KC:
                    proj(c + 1)
                    yield
                conv(c)
                yield
            rV = [("vc", c) for c in range(KC)]
            dve(lambda e: e.tensor_copy(out=sq[:, :, 0:n], in_=vcb[:, :, 0:n]), rV, [("sq",)])
            b = psb("c")
            mm(b, n, [(onesm, sq[:, c, 0:n]) for c in range(KC)], [("sq",), ("cst",)])
            yield
            for c in range(KC):
                dve(lambda e, c=c, b=b: e.tensor_tensor(out=vcb[:, c, 0:n], in0=vcb[:, c, 0:n], in1=ps[b][:, 0:n],
                                                        op=ALU.subtract),
                    [("vc", c), ("ps", b)], [("vc", c)])
            yield
            act(sq[:, :, 0:n], vcb[:, :, 0:n], AF.Square, rV, [("sq",)])
            b2 = psb("c")
            mm(b2, n, [(onesm, sq[:, c, 0:n]) for c in range(KC)], [("sq",), ("cst",)])
            yield
            act(stat[:, 2, 0:n], ps[b2][:, 0:n], AF.Sqrt, [("ps", b2)], [("stat", 2)], bias=EPS)
            dve(lambda e: e.reciprocal(out=stat[:, 3, 0:n], in_=stat[:, 2, 0:n]), [("stat", 2)], [("stat", 3)])
            yield
            for c in range(KC):
                dve(lambda e, c=c: e.scalar_tensor_tensor(out=vcb[:, c, 0:n], in0=vcb[:, c, 0:n],
                                                          scalar=pcol("lng", c), in1=stat[:, 3, 0:n],
                                                          op0=ALU.mult, op1=ALU.mult),
                    [("vc", c), ("stat", 3), ("prm",)], [("vc", c)])
                act(hid[:, 24 + c, 0:n], vcb[:, c, 0:n], AF.Silu, [("vc", c), ("prm",)], [("hid", 24 + c)],
                    bias=pcol("lnb", c))
                if c % 2 == 1:
                    yield

        def interleave(*gens):
            gens = list(gens)
            while gens:
                for g in list(gens):
                    try:
                        next(g)
                    except StopIteration:
                        gens.remove(g)

        def mixer_rest(i):
            n = TS[i]
            nx = n + HALO
            lo, hi = MS[i], MS[i] + n
            ga = gen_rnn_branch(i)
            for _ in range(20):
                next(ga)
            interleave(ga, gen_conv_branch(i))
            zk = [("hid", 24 + c) for c in range(KC)]
            for m in range(KC):
                k1 = load_m(wc_v[m])
                k2 = load_m(win_v[40 + m])
                ba, bb = psb("a"), psb("b")
                mm(ba, n, [(slots[:, k1, kc * 128:(kc + 1) * 128], hid[:, 24 + kc, 0:n]) for kc in range(KC)],
                   zk + [("slot", k1)])
                mm(bb, n, [(slots[:, k2, kc * 128:(kc + 1) * 128], xn[:, kc, HALO:nx]) for kc in range(KC)],
                   xnk + [("slot", k2)])
                tj = m % 4
                act(tmp[:, tj, 0:n], ps[bb][:, 0:n], AF.Sigmoid, [("ps", bb), ("prm",)], [("tmp", tj)],
                    bias=pcol("bin", 40 + m))
                dve(lambda e, m=m, tj=tj, ba=ba: e.scalar_tensor_tensor(
                    out=tmp[:, tj, 0:n], in0=ps[ba][:, 0:n], scalar=pcol("bp", m), in1=tmp[:, tj, 0:n],
                    op0=ALU.add, op1=ALU.mult),
                    [("tmp", tj), ("ps", ba), ("prm",)], [("tmp", tj)])
                dve(lambda e, m=m, tj=tj: e.tensor_tensor(out=hid[:, 16 + m, 0:n], in0=tmp[:, tj, 0:n],
                                                          in1=hid[:, 16 + m, 0:n], op=ALU.add),
                    [("tmp", tj), ("hid", 16 + m)], [("hid", 16 + m)])
            mk = [("hid", 16 + c) for c in range(KC)]
            for m in range(KC):
                k = load_m(wo_v[m])
                b = psb("a")
                mm(b, n, [(slots[:, k, kc * 128:(kc + 1) * 128], hid[:, 16 + kc, 0:n]) for kc in range(KC)],
                   mk + [("slot", k)])
                dve(lambda e, m=m, b=b: e.tensor_tensor(out=Bh[:, m, lo:hi], in0=ps[b][:, 0:n], in1=Bh[:, m, lo:hi],
                                                        op=ALU.add),
                    [("ps", b)] + hkeys([i], [m]), hkeys([i], [m]))

        outTv = outT.rearrange("(c p) t -> p c t", p=128)

        def final_out(i):
            n = TS[i]
            lo, hi = MS[i], MS[i] + n
            rmsnorm_to(lo, hi, "gf", lambda c: vcb[:, c, 0:n], lambda c: ("vc", c))
            for c in range(KC):
                S.dma("sp", lambda e, c=c: e.dma_start(out=outTv[:, c, lo - HALO:hi - HALO], in_=vcb[:, c, 0:n]),
                      "s_out%d" % c, reads=[("vc", c)], writes=[("out", i, c)])

        ffn_phase(0, "g1", True)
        S.barrier()
        def adv(g, k):
            for _ in range(k):
                next(g)

        pre_g = {}

        def head_gen(i):
            mixnorm(i)
            yield
            g = rnn_front(i, False)
            pre_g[i] = g
            for _ in range(KC):
                next(g)
                yield

        for _ in head_gen(0):
            pass
        for i in range(NT):
            g = pre_g[i]
            adv(g, 12)
            if i + 1 < NT:
                interleave(g, head_gen(i + 1))
            else:
                for _ in g:
                    pass
        S.dma("pool", lambda e: e.dma_start(out=cc_in.ap(), in_=state[:]), "s_cc1", reads=[("state",)],
              writes=[("cc_in",)])

        def cc_fn(e):
            return e.collective_compute("AllGather", ALU.bypass, replica_groups=[list(range(NCORES))],
                                        ins=[cc_in.ap().opt()], outs=[cc_out.ap().opt()])
        S.dma("pool", cc_fn, "s_cc2", reads=[("cc_in",)], writes=[("cc_out",)], inc=1)
        for r in range(NCORES):
            S.dma("pool", lambda e, r=r: e.dma_start(out=gath[:, r, :], in_=cc_out.ap()[r * 128:(r + 1) * 128, :]),
                  "s_cc3", reads=[("cc_out",)], writes=[("gath",)] if r == NCORES - 1 else [("gath_part", r)])
        def combine_carry():
            dve(lambda e: e.memset(state[:], 0.0), [("state",)], [("state",)])
            for r in range(NCORES):
                dve(lambda e, r=r: e.scalar_tensor_tensor(out=state[:], in0=gath[:, r, :], scalar=pcol("mask", r),
                                                          in1=state[:], op0=ALU.mult, op1=ALU.add),
                    [("gath",), ("state",), ("prm",)], [("state",)])
        for i in range(NT):
            mixnorm(i, True)
            mixer_rest(i)
        S.barrier()
        ffn_phase(1, "g2", False)
        for i in range(NT):
            final_out(i)
        S.wait_all("sp", ["s_out%d" % c for c in range(KC)])

        S.finalize()
        sems = {}
        for name in sorted(S.sem_names):
            sems[name] = es.enter_context(nc.semaphore(name))
        with nc.Block() as block:
            @block.sync
            def _(e):
                S.replay("sp", e, sems)

            @block.gpsimd
            def _(e):
                S.replay("pool", e, sems)

            @block.scalar
            def _(e):
                S.replay("act", e, sems)

            @block.vector
            def _(e):
                S.replay("dve", e, sems)

            @block.tensor
            def _(e):
                S.replay("pe", e, sems)
    return nc


def _vec_cols(v):
    v = np.asarray(v, np.float32).reshape(-1)
    return np.ascontiguousarray(v.reshape(-1, 128).T)


def _kmajor(w):
    K, N = w.shape
    a = w.reshape(K // 128, 128, N // 128, 128)
    return np.ascontiguousarray(a.transpose(2, 1, 0, 3).reshape(N // 128, 128, (K // 128) * 128))


_CACHE = {}


def kernel(x, meta_tokens, ffn1_norm, ffn1_w_gu, ffn1_w_down, mix_norm, w_in, b_in,
           rnn_conv_w, rnn_conv_b, rg_w_a, rg_b_a, rg_w_x, rg_b_x, rg_lambda, rnn_w_proj,
           conv_dw_w, conv_dw_b, conv_ln_g, conv_ln_b, conv_w_proj, conv_b_proj, w_out,
           ffn2_norm, ffn2_w_gu, ffn2_w_down, final_norm):
    f = lambda a: np.asarray(a, np.float32)
    x = f(x)
    B = x.shape[0]
    meta = f(meta_tokens)
    NMETA = meta.shape[0]
    shared = {}
    for nm, wgu_, wd_ in (("1", f(ffn1_w_gu)[0], f(ffn1_w_down)[0]), ("2", f(ffn2_w_gu)[0], f(ffn2_w_down)[0])):
        g = _kmajor(wgu_[:, :DFF])
        u = _kmajor(wgu_[:, DFF:])
        shared["wgu" + nm] = np.ascontiguousarray(np.stack([g, u], axis=1))
        shared["wd" + nm] = _kmajor(wd_)
    shared["win"] = _kmajor(f(w_in)[0])
    wa, wx = f(rg_w_a)[0], f(rg_w_x)[0]
    wg = np.stack([wa, wx], axis=1)
    wg = wg.reshape(4, 2, 2, 128, 256).transpose(0, 3, 1, 2, 4)
    shared["wgate"] = np.ascontiguousarray(wg.reshape(4, 128, 1024))
    shared["wp"] = _kmajor(f(rnn_w_proj)[0])
    shared["wc"] = _kmajor(f(conv_w_proj)[0])
    shared["wo"] = _kmajor(f(w_out)[0])
    consts = np.zeros((128, 256), np.float32)
    consts[:, :128] = np.eye(128, dtype=np.float32)
    consts[:, 128:] = 1.0 / D
    shared["consts"] = consts
    pbase = np.zeros((128, NP), np.float32)

    def put(name, v):
        c = _vec_cols(v)
        pbase[:, PC[name]:PC[name] + c.shape[1]] = c

    put("g1", ffn1_norm); put("gm", mix_norm); put("g2", ffn2_norm); put("gf", final_norm)
    put("bin", b_in)
    put("w4", f(rnn_conv_w)[0].reshape(-1)); put("cb4", rnn_conv_b)
    put("ba", rg_b_a); put("bx", rg_b_x); put("lam", rg_lambda)
    put("w31", f(conv_dw_w)[0].reshape(-1)); put("cb31", conv_dw_b)
    put("lng", conv_ln_g); put("lnb", conv_ln_b); put("bp", conv_b_proj)

    in_maps = []
    nA = TMAIN - NMETA
    for core in range(NCORES):
        b, half = core // 2, core % 2
        tok = np.zeros((T, D), np.float32)
        p = pbase.copy()
        if half == 0:
            tok[HALO:HALO + NMETA] = meta
            tok[HALO + NMETA:] = x[b, :nA]
            p[:, PC["flag"]] = 0.0
        else:
            tok[:] = x[b, nA - HALO:]
            p[:, PC["flag"]] = 1.0
            p[:, PC["mask"] + core - 1] = 1.0
        m = dict(shared)
        m["xT"] = np.ascontiguousarray(tok.T)
        m["params"] = p
        in_maps.append(m)

    if "nc" not in _CACHE:
        _CACHE["nc"] = build_program()
    res = run_bass_kernel_spmd(_CACHE["nc"], in_maps, core_ids=list(range(NCORES)))
    out = np.empty((B, x.shape[1], D), np.float32)
    for core in range(NCORES):
        b, half = core // 2, core % 2
        o = np.asarray(res.results[core]["outT"]).T
        if half == 0:
            out[b, :nA] = o[NMETA:]
        else:
            out[b, nA:] = o
    return out
```
